# Optimizing a Trainium2 kernel written in Bass

```python
import math
import jax
import jax.numpy as jnp
from jax import lax
import numpy as np


D_MODEL = 1024
BATCH = 4
SEQ = 4096
DEPTH = 2
DEC_BATCH = 2
DEC_SEQ = 16384
PAST_LEN = 128

GRID_W = 64
PLE_DIM = 256
D_FF = 2816
QBLK = 128
NORM_EPS = 1e-6
ROPE_THETA = 10000.0
NEG_INF = -1e30

HA = 8
Q_RANK = 256
KV_RANK = 128
NOPE_A = 64
ROPE_A = 32
V_A = 64
QK_A = NOPE_A + ROPE_A

HD = 64
DIL_PAIRS = ((128, 1), (512, 4), (2048, 16))
N_GROUPS_B = 3
HPG_B = 4
N_HEADS_B = N_GROUPS_B * HPG_B
T5_BUCKETS = 32
T5_MAX_DIST = 1024

HC = 8
KVC = 2
GC = HC // KVC

N_BRANCH = 3
A_COLS = Q_RANK + KV_RANK + ROPE_A
B_COLS = 3 * N_HEADS_B * HD
C_COLS = (HC + 2 * KVC) * HD
IN_COLS = A_COLS + B_COLS + C_COLS

kernel_name = 'hybrid_gated_mla_dilated_axialgqa_encoder'


def rmsnorm(x, g):
    xf = x.astype(jnp.float32)
    y = xf * lax.rsqrt(jnp.mean(xf * xf, axis=-1, keepdims=True) + NORM_EPS)
    return (y * g.astype(jnp.float32)).astype(x.dtype)


def swiglu(x, w_in, w_out):
    a, b = jnp.split(x @ w_in, 2, axis=-1)
    return (jax.nn.silu(a) * b) @ w_out


def rope(x, pos):
    half = x.shape[-1] // 2
    freqs = (ROPE_THETA ** (-np.arange(half) / half)).astype(np.float32)
    ang = pos.astype(jnp.float32)[:, None] * jnp.asarray(freqs)[None, :]
    cos = jnp.cos(ang)[:, None, :].astype(x.dtype)
    sin = jnp.sin(ang)[:, None, :].astype(x.dtype)
    x1, x2 = x[..., :half], x[..., half:]
    return jnp.concatenate([x1 * cos - x2 * sin, x1 * sin + x2 * cos], axis=-1)


def dense_attn(q, k, v, scale):
    B, S, Hk, G, dk = q.shape
    nq = S // QBLK
    qb = q.reshape(B, nq, QBLK, Hk, G, dk).transpose(1, 0, 2, 3, 4, 5)

    def one_block(qi):
        s = jnp.einsum('bqhgd,bkhd->bhgqk', qi, k).astype(jnp.float32) * scale
        p = jax.nn.softmax(s, axis=-1).astype(v.dtype)
        return jnp.einsum('bhgqk,bkhd->bqhgd', p, v)

    o = lax.map(one_block, qb)
    return o.transpose(1, 0, 2, 3, 4, 5).reshape(B, S, Hk, G, v.shape[-1])


def t5_bucket(rel):
    nb = T5_BUCKETS // 2
    max_exact = nb // 2
    n = np.abs(rel)
    large = max_exact + (np.log(np.maximum(n, 1) / max_exact) / math.log(T5_MAX_DIST / max_exact) * (nb - max_exact)).astype(np.int32)
    large = np.minimum(large, nb - 1)
    return (rel > 0).astype(np.int32) * nb + np.where(n < max_exact, n, large).astype(np.int32)


def local_attn(q, k, v, bias, R):
    N, L, H, dh = q.shape
    nb = -(-L // R)
    Lp = nb * R
    qp = jnp.pad(q, ((0, 0), (0, Lp - L), (0, 0), (0, 0))).reshape(N, nb, R, H, dh)
    kpad = ((0, 0), (R, Lp - L + R), (0, 0), (0, 0))
    kb = jnp.pad(k, kpad).reshape(N, nb + 2, R, H, dh)
    vb = jnp.pad(v, kpad).reshape(N, nb + 2, R, H, dh)
    kw = jnp.concatenate([kb[:, :-2], kb[:, 1:-1], kb[:, 2:]], axis=2)
    vw = jnp.concatenate([vb[:, :-2], vb[:, 1:-1], vb[:, 2:]], axis=2)
    qpos = np.arange(Lp).reshape(nb, R)[:, :, None]
    kpos = (np.arange(nb)[:, None] * R - R + np.arange(3 * R)[None, :])[:, None, :]
    valid = (np.abs(kpos - qpos) <= R) & (kpos >= 0) & (kpos < L)
    s = jnp.einsum('nbqhd,nbkhd->nbhqk', qp, kw).astype(jnp.float32) * (dh ** -0.5) + bias
    s = jnp.where(jnp.asarray(valid)[None, :, None], s, NEG_INF)
    lse = jax.nn.logsumexp(s, axis=-1)
    p = jnp.exp(s - lse[..., None]).astype(v.dtype)
    o = jnp.einsum('nbhqk,nbkhd->nbqhd', p, vw).reshape(N, Lp, H, dh)[:, :L]
    lse = lse.transpose(0, 1, 3, 2).reshape(N, Lp, H)[:, :L]
    return o, lse


def dilated_group(q, k, v, tab, window, dil):
    B, S, H, dh = q.shape
    R = window // (2 * dil)
    L = S // dil

    def split(t):
        return t.reshape(B, L, dil, H, dh).transpose(0, 2, 1, 3, 4).reshape(B * dil, L, H, dh)

    qi = np.arange(R)[:, None]
    kj = np.arange(3 * R)[None, :]
    bucket = t5_bucket((kj - R - qi) * dil)
    bias = jnp.transpose(tab[bucket], (2, 0, 1)).astype(jnp.float32)
    o, lse = local_attn(split(q), split(k), split(v), bias, R)
    o = o.reshape(B, dil, L, H, dh).transpose(0, 2, 1, 3, 4).reshape(B, S, H, dh)
    lse = lse.reshape(B, dil, L, H).transpose(0, 2, 1, 3).reshape(B, S, H)
    return o, lse


def mla_mixer(a_in, pos, g_cq, g_ckv, w_uq, w_ukv, g_q, g_k):
    B, S, _ = a_in.shape
    c_q = rmsnorm(a_in[..., :Q_RANK], g_cq)
    c_kv = rmsnorm(a_in[..., Q_RANK:Q_RANK + KV_RANK], g_ckv)
    k_rope = a_in[..., Q_RANK + KV_RANK:][:, :, None, :]
    q = (c_q @ w_uq).reshape(B, S, HA, QK_A)
    kv = (c_kv @ w_ukv).reshape(B, S, HA, NOPE_A + V_A)
    q_nope = rmsnorm(q[..., :NOPE_A], g_q[:NOPE_A])
    q_rope = rope(rmsnorm(q[..., NOPE_A:], g_q[NOPE_A:]), pos)
    k_nope = rmsnorm(kv[..., :NOPE_A], g_k[:NOPE_A])
    k_rope = rope(rmsnorm(k_rope, g_k[NOPE_A:]), pos)
    q = jnp.concatenate([q_nope, q_rope], axis=-1)
    k = jnp.concatenate([k_nope, jnp.broadcast_to(k_rope, (B, S, HA, ROPE_A))], axis=-1)
    v = kv[..., NOPE_A:]
    o = dense_attn(q[:, :, :, None, :], k, v, QK_A ** -0.5)
    return o.reshape(B, S, HA * V_A)


def dilated_mixer(b_in, g_q, g_k, rel_bias):
    B, S, _ = b_in.shape
    qkv = b_in.reshape(B, S, 3, N_GROUPS_B, HPG_B, HD)
    q = rmsnorm(qkv[:, :, 0], g_q)
    k = rmsnorm(qkv[:, :, 1], g_k)
    v = qkv[:, :, 2]
    outs = []
    lses = []
    for g, (window, dil) in enumerate(DIL_PAIRS):
        tab = rel_bias[:, g * HPG_B:(g + 1) * HPG_B]
        o, l = dilated_group(q[:, :, g], k[:, :, g], v[:, :, g], tab, window, dil)
        outs.append(o)
        lses.append(l)
    outs = jnp.stack(outs, axis=0)
    lses = jnp.stack(lses, axis=0)
    alpha = jax.nn.softmax(lses, axis=0).astype(outs.dtype)
    o = jnp.sum(alpha[..., None] * outs, axis=0)
    return o.reshape(B, S, HPG_B * HD)


def axial_rope(t, row_pos, col_pos):
    h = t.shape[-1] // 2
    return jnp.concatenate([rope(t[..., :h], row_pos), rope(t[..., h:], col_pos)], axis=-1)


def axial_gqa_mixer(c_in, row_pos, col_pos, g_q, g_k):
    B, S, _ = c_in.shape
    q = c_in[..., :HC * HD].reshape(B, S, HC, HD)
    k = c_in[..., HC * HD:(HC + KVC) * HD].reshape(B, S, KVC, HD)
    v = c_in[..., (HC + KVC) * HD:].reshape(B, S, KVC, HD)
    q = axial_rope(rmsnorm(q, g_q), row_pos, col_pos)
    k = axial_rope(rmsnorm(k, g_k), row_pos, col_pos)
    o = dense_attn(q.reshape(B, S, KVC, GC, HD), k, v, HD ** -0.5)
    return o.reshape(B, S, HC * HD)


def _layer(x, pe, W, i, pos, row_pos, col_pos):
    B, S, _ = x.shape
    x = x + 0.5 * swiglu(rmsnorm(x, W['g_ffn1'][i]), W['w_ffn1_in'][i], W['w_ffn1_out'][i])
    u = rmsnorm(x, W['g_mix'][i])
    proj = u @ W['w_in'][i]
    a_in = proj[..., :A_COLS]
    b_in = proj[..., A_COLS:A_COLS + B_COLS]
    c_in = proj[..., A_COLS + B_COLS:]
    o_a = mla_mixer(a_in, pos, W['g_cq'][i], W['g_ckv'][i], W['w_uq'][i], W['w_ukv'][i], W['g_qa'][i], W['g_ka'][i])
    o_b = dilated_mixer(b_in, W['g_qb'][i], W['g_kb'][i], W['rel_bias'])
    o_c = axial_gqa_mixer(c_in, row_pos, col_pos, W['g_qc'][i], W['g_kc'][i])
    gates = jax.nn.sigmoid(u @ W['w_gate'][i] + W['b_gate'][i]).reshape(B, S, N_BRANCH, D_MODEL)
    merged = (gates[:, :, 0] * (o_a @ W['w_oa'][i])
              + gates[:, :, 1] * (o_b @ W['w_ob'][i])
              + gates[:, :, 2] * (o_c @ W['w_oc'][i]))
    x = x + merged @ W['w_out'][i]
    x = x + 0.5 * swiglu(rmsnorm(x, W['g_ffn2'][i]), W['w_ffn2_in'][i], W['w_ffn2_out'][i])
    g = jax.nn.sigmoid(rmsnorm(x, W['g_ple'][i]) @ W['w_pg'][i])
    return x + g * (pe @ W['w_ple'][i])


def _trunk(x, p, W):
    S = x.shape[1]
    rows = S // GRID_W
    pos = jnp.arange(S, dtype=jnp.int32)
    row_pos = jnp.repeat(jnp.arange(rows, dtype=jnp.int32), GRID_W)
    col_pos = pos % GRID_W
    for i in range(DEPTH):
        x = _layer(x, p[i], W, i, pos, row_pos, col_pos)
    return x


def setup_inputs(seed: int = 0) -> dict:
    key = jax.random.key(seed)
    keys = jax.random.split(key, 32)
    cnt = [0]

    def nk():
        k = keys[cnt[0]]
        cnt[0] += 1
        return k

    def nrm(shape, fan_in):
        return jax.random.normal(nk(), shape, jnp.float32) * (fan_in ** -0.5)

    def gain(shape):
        return 1.0 + 0.05 * jax.random.normal(nk(), shape, jnp.float32)

    L = DEPTH
    return {
        'x_prompt': jax.random.normal(nk(), (BATCH, SEQ, D_MODEL), jnp.float32),
        'x_sample': jax.random.normal(nk(), (DEC_BATCH, DEC_SEQ, D_MODEL), jnp.float32),
        'p_prompt': jax.random.normal(nk(), (DEPTH, BATCH, SEQ, PLE_DIM), jnp.float32),
        'p_sample': jax.random.normal(nk(), (DEPTH, DEC_BATCH, DEC_SEQ, PLE_DIM), jnp.float32),
        'g_ffn1': gain((L, D_MODEL)),
        'w_ffn1_in': nrm((L, D_MODEL, 2 * D_FF), D_MODEL),
        'w_ffn1_out': nrm((L, D_FF, D_MODEL), D_FF),
        'g_mix': gain((L, D_MODEL)),
        'w_in': nrm((L, D_MODEL, IN_COLS), D_MODEL),
        'g_cq': gain((L, Q_RANK)),
        'g_ckv': gain((L, KV_RANK)),
        'w_uq': nrm((L, Q_RANK, HA * QK_A), Q_RANK),
        'w_ukv': nrm((L, KV_RANK, HA * (NOPE_A + V_A)), KV_RANK),
        'g_qa': gain((L, QK_A)),
        'g_ka': gain((L, QK_A)),
        'g_qb': gain((L, HD)),
        'g_kb': gain((L, HD)),
        'rel_bias': 0.5 * jax.random.normal(nk(), (T5_BUCKETS, N_HEADS_B), jnp.float32),
        'g_qc': gain((L, HD)),
        'g_kc': gain((L, HD)),
        'w_gate': nrm((L, D_MODEL, N_BRANCH * D_MODEL), D_MODEL),
        'b_gate': 0.1 * jax.random.normal(nk(), (L, N_BRANCH * D_MODEL), jnp.float32),
        'w_oa': nrm((L, HA * V_A, D_MODEL), HA * V_A),
        'w_ob': nrm((L, HPG_B * HD, D_MODEL), HPG_B * HD),
        'w_oc': nrm((L, HC * HD, D_MODEL), HC * HD),
        'w_out': nrm((L, D_MODEL, D_MODEL), D_MODEL),
        'g_ffn2': gain((L, D_MODEL)),
        'w_ffn2_in': nrm((L, D_MODEL, 2 * D_FF), D_MODEL),
        'w_ffn2_out': nrm((L, D_FF, D_MODEL), D_FF),
        'g_ple': gain((L, D_MODEL)),
        'w_pg': nrm((L, D_MODEL, D_MODEL), D_MODEL),
        'w_ple': nrm((L, PLE_DIM, D_MODEL), PLE_DIM),
    }


def reference(x_prompt, x_sample, p_prompt, p_sample, g_ffn1, w_ffn1_in, w_ffn1_out, g_mix, w_in,
              g_cq, g_ckv, w_uq, w_ukv, g_qa, g_ka, g_qb, g_kb, rel_bias, g_qc, g_kc,
              w_gate, b_gate, w_oa, w_ob, w_oc, w_out, g_ffn2, w_ffn2_in, w_ffn2_out,
              g_ple, w_pg, w_ple):
    W = dict(g_ffn1=g_ffn1, w_ffn1_in=w_ffn1_in, w_ffn1_out=w_ffn1_out, g_mix=g_mix, w_in=w_in,
             g_cq=g_cq, g_ckv=g_ckv, w_uq=w_uq, w_ukv=w_ukv, g_qa=g_qa, g_ka=g_ka,
             g_qb=g_qb, g_kb=g_kb, rel_bias=rel_bias, g_qc=g_qc, g_kc=g_kc,
             w_gate=w_gate, b_gate=b_gate, w_oa=w_oa, w_ob=w_ob, w_oc=w_oc, w_out=w_out,
             g_ffn2=g_ffn2, w_ffn2_in=w_ffn2_in, w_ffn2_out=w_ffn2_out,
             g_ple=g_ple, w_pg=w_pg, w_ple=w_ple)
    y_prompt = _trunk(x_prompt, p_prompt, W)
    y_sample = _trunk(x_sample, p_sample, W)
    return (y_prompt, y_sample)
```

```python
import contextlib
import numpy as np
import concourse.bass as bass
import concourse.mybir as mybir
from concourse.bass_utils import run_bass_kernel_spmd

F32, BF16, I32 = mybir.dt.float32, mybir.dt.bfloat16, mybir.dt.int32
AF = mybir.ActivationFunctionType
ALU = mybir.AluOpType
AX = mybir.AxisListType

NCORES = 8
DEPTH = 2
D = 1024
DFF = 2816
NF = 22
T = 6144
TB = 512
NBLK = T // TB
NSEQ = 6
SEQ_N = [512, 512, 512, 512, 2048, 2048]
SEQ_S = [4096, 4096, 4096, 4096, 16384, 16384]
SEQ_OFF = [0, 512, 1024, 1536, 2048, 4096]
EPS = 1e-6
QA0, QC0, QB0 = 0, 768, 1280
KB0, KAN0, KAR0, KC0 = 0, 768, 1280, 1312
NQ, NK = 2048, 1536
NPOST = NQ + NK
VA0, VC0, VB0 = 0, 512, 640
NV = 1408
G_CQ, G_CKV, G_B, G_C, G_QA, G_KA = 0, 256, 384, 1920, 2560, 2656
NGAIN = 2752
DIL = [(1, 1), (4, 2), (16, 8)]
GW = [(2 * nh + 1) * 128 for _, nh in DIL]
GOFF = [0, 384, 1024]
GTOT = 3200
HALO = 1024
NEXT_CH = [n // 128 + 16 for n in SEQ_N]
KV_OFF = [0]
for _n in NEXT_CH:
    KV_OFF.append(KV_OFF[-1] + _n)


class Sem:
    _n = 0

    def __init__(self, nc, name):
        Sem._n += 1
        self.id = Sem._n
        self.h = nc.alloc_semaphore(f"{name}_{self.id}")


class Buf:
    __slots__ = ("name", "w", "r", "dsem", "dcnt", "dram")

    def __init__(self, name, dram=False):
        self.name = name
        self.w = None
        self.r = {}
        self.dsem = None
        self.dcnt = 0
        self.dram = dram


class KB:
    def __init__(self, nc):
        self.nc = nc
        self.eng = {"pe": nc.tensor, "act": nc.scalar, "dve": nc.vector, "pool": nc.gpsimd, "sp": nc.sync}
        self.sem = {e: Sem(nc, "s" + e) for e in ("pe", "act", "dve", "pool")}
        self.cnt = {e: 0 for e in self.sem}
        self.known = {e: {} for e in self.eng}
        self.dbufs = []
        self.ps_next = 0
        self.sem_pool = []
        self.phase_bufs = []
        self.allsem = {}

    def _wait(self, E, deps):
        kn = self.known[E]
        best = {}
        for (s, v) in deps:
            if E == "pe" and s is self.sem["pe"]:
                continue
            if kn.get(s.id, 0) < v and (s.id not in best or best[s.id][1] < v):
                best[s.id] = (s, v)
        for sid, (s, v) in best.items():
            self.eng[E].wait_ge(s.h, v)
            kn[sid] = v

    def _deps(self, reads, writes, part=False):
        deps = []
        for b in reads:
            if b.w is not None:
                deps.append(b.w)
        for b in writes:
            if b.dram:
                continue
            if not part:
                deps.extend(b.r.values())
            if b.w is not None and not part:
                deps.append(b.w)
        return deps

    def op(self, E, fn, reads=(), writes=(), inc=True):
        self._wait(E, self._deps(reads, writes))
        ins = fn(self.eng[E])
        s = self.sem[E]
        if inc:
            self.cnt[E] += 1
            ins.then_inc(s.h, 1)
            tok = (s, self.cnt[E])
        else:
            tok = (s, self.cnt[E] + 1)
        for b in reads:
            b.r[s.id] = tok
        for b in writes:
            b.w = tok
            b.r = {}
        return ins

    def dma(self, Q, out, in_, reads=(), writes=(), part=False, **kw):
        self._wait(Q, self._deps(reads, writes, part))
        ins = self.eng[Q].dma_start(out=out, in_=in_, **kw)
        wb = writes[0]
        if wb.dsem is None:
            if self.sem_pool:
                wb.dsem, wb.dcnt = self.sem_pool.pop()
            else:
                wb.dsem = Sem(self.nc, "d" + wb.name[:8])
        wb.dcnt += 16
        ins.then_inc(wb.dsem.h, 16)
        tok = (wb.dsem, wb.dcnt)
        self.allsem[wb.dsem.id] = tok
        for b in reads:
            b.r[wb.dsem.id] = tok
        for b in writes:
            b.w = tok
            if not b.dram:
                b.r = {}
        return ins

    def coll(self, fn, reads, writes):
        self._wait("pool", self._deps(reads, writes))
        if not hasattr(self, "ccsem"):
            self.ccsem = Sem(self.nc, "cc")
            self.cccnt = 0
        ins = fn(self.eng["pool"])
        self.cccnt += 1
        ins.then_inc(self.ccsem.h, 1)
        tok = (self.ccsem, self.cccnt)
        self.allsem[self.ccsem.id] = tok
        for b in reads:
            b.r[self.ccsem.id] = tok
        for b in writes:
            b.w = tok
        return ins

    def barrier(self):
        toks = [(self.sem[e], self.cnt[e]) for e in self.sem if self.cnt[e] > 0]
        toks += list(self.allsem.values())
        for E in self.eng:
            self._wait(E, toks)

    def rotate_sems(self):
        self.barrier()
        for e in list(self.sem):
            self.sem[e] = Sem(self.nc, "s" + e)
            self.cnt[e] = 0

    def end_phase(self):
        self.barrier()
        for b in self.phase_bufs:
            if b.dsem is not None:
                self.sem_pool.append((b.dsem, b.dcnt))
                b.dsem = None
        self.phase_bufs = []


def build(stage="full", nlayers=DEPTH):
    nc = bass.Bass("TRN2", target_bir_lowering=False)
    kb = KB(nc)

    def din(name, shape, dt=F32):
        return nc.dram_tensor(name, list(shape), dt, kind="ExternalInput").ap()

    def dout(name, shape, dt=F32):
        return nc.dram_tensor(name, list(shape), dt, kind="ExternalOutput").ap()

    def dscr(name, shape, dt=BF16):
        return nc.dram_tensor(name, list(shape), dt).ap()

    x_in = din("x_loc", [T, D])
    pT_in = din("pT_loc", [DEPTH, 256, T])
    rope_in = din("rope_tab", [T, 96])
    ident_in = din("ident", [128, 128])
    gb_in = din("gb_tab", [128, 4, GTOT])
    kvalid_in = din("kvalid", [128, KV_OFF[-1]])
    hidx_in = din("hidx", [1, 24], I32)
    gcols_in = din("gcols", [DEPTH, 128, 32])
    bgate_in = din("bgate_cols", [DEPTH, 128, 24])
    gains_in = din("gains_row", [DEPTH, NGAIN])
    w = {}
    for nm, shp in [("w_ffn1_in", [D, 2 * DFF]), ("w_ffn1_out", [DFF, D]), ("w_in", [D, 3488]),
                    ("w_uq", [256, 768]), ("w_ukv", [128, 1024]), ("w_gate", [D, 3 * D]),
                    ("w_oa", [512, D]), ("w_ob", [256, D]), ("w_oc", [512, D]), ("w_out", [D, D]),
                    ("w_ffn2_in", [D, 2 * DFF]), ("w_ffn2_out", [DFF, D]), ("w_pg", [D, D]),
                    ("w_ple", [256, D])]:
        w[nm] = din(nm, [DEPTH] + shp)
    y_out = dout("y_loc", [T, D])
    dbg = {}

    L = nlayers
    W1in = [dscr(f"W1in{l}", [NF, 128, 8, 256]) for l in range(L)]
    W2in = [dscr(f"W2in{l}", [NF, 128, 8, 256]) for l in range(L)]
    W1out = [dscr(f"W1out{l}", [2, NF, 128, 512]) for l in range(L)]
    W2out = [dscr(f"W2out{l}", [2, NF, 128, 512]) for l in range(L)]
    Win = [dscr(f"Win{l}", [7, 128, 8, 512]) for l in range(L)]
    Wgate = [dscr(f"Wgate{l}", [24, 128, 8, 128]) for l in range(L)]
    Wbr = [dscr(f"Wbr{l}", [8, 128, 10, 128]) for l in range(L)]
    Wout = [dscr(f"Wout{l}", [2, 128, 8, 512]) for l in range(L)]
    Wpg = [dscr(f"Wpg{l}", [2, 128, 8, 512]) for l in range(L)]
    Wple = [dscr(f"Wple{l}", [2, 128, 2, 512]) for l in range(L)]
    Wuq = [dscr(f"Wuq{l}", [128, 2, 768]) for l in range(L)]
    Wukv = [dscr(f"Wukv{l}", [128, 1024]) for l in range(L)]
    pTb = [dscr(f"pTb{l}", [256, T]) for l in range(L)]
    xs1 = [dscr(f"xs1_{l}", [T, D], F32) for l in range(L)]
    QT = [dscr(f"QT{l}", [NQ, T]) for l in range(L)]
    Xin = [dscr(f"Xin{l}", [1536, T]) for l in range(L)]
    Din = [dscr(f"Din{l}", [1408, T]) for l in range(L)]
    Xout = dscr("Xout", [NCORES * 1536, T])
    Dout = dscr("Dout", [NCORES * 1408, T])
    oT = [dscr(f"oT{l}", [1280, T]) for l in range(L)]
    B_wprep = [Buf(f"wprep{l}", dram=True) for l in range(L)]
    B_xs1 = [Buf(f"xs1{l}", dram=True) for l in range(L)]
    B_QT = [Buf(f"QT{l}", dram=True) for l in range(L)]
    B_gKin = [Buf(f"gKin{l}", dram=True) for l in range(L)]
    B_gVin = [Buf(f"gVin{l}", dram=True) for l in range(L)]
    B_gKout = [Buf(f"gKout{l}", dram=True) for l in range(L)]
    B_gVout = [Buf(f"gVout{l}", dram=True) for l in range(L)]
    B_oT = [Buf(f"oT{l}", dram=True) for l in range(L)]
    nbr = [dscr(f"nbr_{k}", [1536, T]) for k in range(4)]
    nbrK = [[n_[0:768, :] for n_ in nbr]] * L
    nbrV = [[n_[768:1536, :].rearrange("a b -> (a b)") for n_ in nbr]] * L
    B_nbr = [Buf(f"nbr{l}", dram=True) for l in range(L)]
    B_y = Buf("y", dram=True)
    B_in = Buf("inputs", dram=True)

    def prep_blocks(l, src, dst, C, nbw, ncols, col0=0, dcol0=0):
        nb_n = (ncols + nbw - 1) // nbw
        for nb in range(nb_n):
            wd = min(nbw, ncols - nb * nbw)
            s = src[:, col0 + nb * nbw: col0 + nb * nbw + wd].rearrange("(c p) m -> p c m", p=128)
            kb.dma("pool", dst[nb, :, :, dcol0:dcol0 + wd], s, reads=[B_in], writes=[B_wprep[l]])

    def prep_layer(l):
        for nm, dst in (("w_ffn1_in", W1in[l]), ("w_ffn2_in", W2in[l])):
            prep_blocks(l, w[nm][l], dst, 8, 128, DFF, col0=0, dcol0=0)
            prep_blocks(l, w[nm][l], dst, 8, 128, DFF, col0=DFF, dcol0=128)
        for nm, dst in (("w_ffn1_out", W1out[l]), ("w_ffn2_out", W2out[l])):
            for h in range(2):
                s = w[nm][l][:, h * 512:(h + 1) * 512].rearrange("(f p) m -> p f m", p=128)
                kb.dma("pool", dst[h].rearrange("f p m -> p f m"), s, reads=[B_in], writes=[B_wprep[l]])
        prep_blocks(l, w["w_in"][l], Win[l], 8, 512, 3488)
        prep_blocks(l, w["w_gate"][l], Wgate[l], 8, 128, 3 * D)
        for j in range(8):
            cs = slice(j * 128, (j + 1) * 128)
            kb.dma("pool", Wbr[l][j, :, 0:4, :], w["w_oa"][l][:, cs].rearrange("(c p) m -> p c m", p=128),
                   reads=[B_in], writes=[B_wprep[l]])
            kb.dma("pool", Wbr[l][j, :, 4:6, :], w["w_ob"][l][:, cs].rearrange("(c p) m -> p c m", p=128),
                   reads=[B_in], writes=[B_wprep[l]])
            kb.dma("pool", Wbr[l][j, :, 6:10, :], w["w_oc"][l][:, cs].rearrange("(c p) m -> p c m", p=128),
                   reads=[B_in], writes=[B_wprep[l]])
        prep_blocks(l, w["w_out"][l], Wout[l], 8, 512, D)
        prep_blocks(l, w["w_pg"][l], Wpg[l], 8, 512, D)
        prep_blocks(l, w["w_ple"][l], Wple[l], 2, 512, D)
        kb.dma("pool", Wuq[l], w["w_uq"][l].rearrange("(c p) m -> p c m", p=128), reads=[B_in], writes=[B_wprep[l]])
        kb.dma("pool", Wukv[l], w["w_ukv"][l], reads=[B_in], writes=[B_wprep[l]])
        for h in range(2):
            kb.dma("pool", pTb[l][h * 128:(h + 1) * 128, :], pT_in[l, h * 128:(h + 1) * 128, :],
                   reads=[B_in], writes=[B_wprep[l]])

    for l in range(L):
        prep_layer(l)

    es_top = contextlib.ExitStack()

    sbn = [0]

    def sb(es, name, shape, dt):
        sbn[0] += 1
        t = es.enter_context(nc.sbuf_tensor(f"sb{sbn[0]}_{name}", list(shape), dt))
        b = Buf(name)
        if es is not es_top:
            kb.phase_bufs.append(b)
        return t, b

    ps_all = es_top.enter_context(nc.psum_tensor("ps_all", [128, 4096], F32))
    B_ps = [Buf(f"ps{i}") for i in range(8)]

    def ps_alloc(n=1):
        if kb.ps_next + n > 8:
            kb.ps_next = 0
        b0 = kb.ps_next
        kb.ps_next = (b0 + n) % 8
        return b0, B_ps[b0:b0 + n]

    def psf(b0, n=1):
        return ps_all[:, b0 * 512:(b0 + n) * 512]

    ident_f, B_identf = sb(es_top, "ident_f", [128, 128], F32)
    ident, B_ident = sb(es_top, "ident_b", [128, 128], BF16)
    epsc, B_eps = sb(es_top, "epsc", [128, 1], F32)
    kb.dma("sp", ident_f[:], ident_in, reads=[B_in], writes=[B_identf])
    kb.op("dve", lambda e: e.tensor_copy(out=ident[:], in_=ident_f[:]), reads=[B_identf], writes=[B_ident])
    kb.op("dve", lambda e: e.memset(epsc[:], EPS), writes=[B_eps])

    def mm_group(psb, out_ap, pairs, reads):
        n = len(pairs)
        for i, (lt, rh) in enumerate(pairs):
            kb.op("pe", lambda e, lt=lt, rh=rh, i=i: e.matmul(out_ap, lt, rh, start=(i == 0), stop=(i == n - 1)),
                  reads=reads, writes=psb, inc=(i == n - 1))

    def token_phase(l, do_c, do_a, last):
        la = l + 1 if do_c else l
        with contextlib.ExitStack() as es:
            xb = [sb(es, f"xb{i}", [128, 4, D], F32) for i in range(1)]
            xn, B_xn = sb(es, "xn", [128, D], BF16)
            junk, B_junk = sb(es, "junk", [128, D], BF16)
            ss, B_ss = sb(es, "ss", [128, 8], F32)
            rstd, B_rstd = sb(es, "rstd", [128, 8], F32)
            xT, B_xT = sb(es, "xT", [128, 8, TB], BF16)
            hT, B_hT = sb(es, "hT", [128, NF, TB], BF16)
            tnh = [sb(es, f"tnh{i}", [128, TB], F32) for i in range(2)]
            tmp = [sb(es, f"tmp{i}", [128, TB], F32) for i in range(2)]
            wsl = [sb(es, f"wsl{i}", [128, 4096], BF16) for i in range(3)]
            wctr = [0]
            gcols, B_gcols = sb(es, "gcols", [128, 2, 32], F32)
            bgc, B_bgc = sb(es, "bgc", [128, 24], F32)
            bgh, B_bgh = sb(es, "bgh", [128, 24], F32)
            for ll in range(L):
                kb.dma("sp", gcols[:, ll, :], gcols_in[ll], reads=[B_in], writes=[B_gcols], part=(ll > 0))
            if do_c:
                kb.dma("sp", bgc[:], bgate_in[l], reads=[B_in], writes=[B_bgc])
                kb.op("dve", lambda e: e.tensor_scalar(out=bgh[:], in0=bgc[:], scalar1=0.5, scalar2=None, op0=ALU.mult),
                      reads=[B_bgc], writes=[B_bgh])
            if do_a:
                gains, B_gains = sb(es, "gains", [128, NGAIN], F32)
                kb.dma("sp", gains[:], gains_in[la:la + 1, :].partition_broadcast(128), reads=[B_in], writes=[B_gains])
                wuq, B_wuq = sb(es, "wuq", [128, 2, 768], BF16)
                wukv, B_wukv = sb(es, "wukv", [128, 1024], BF16)
                kb.dma("sp", wuq[:], Wuq[la], reads=[B_wprep[la]], writes=[B_wuq])
                kb.dma("sp", wukv[:], Wukv[la], reads=[B_wprep[la]], writes=[B_wukv])
                proj2, B_proj = sb(es, "proj2", [128, 2, 3488], F32)
                nrm, B_nrm = sb(es, "nrm", [128, 1536], F32)
                sq, B_sq = sb(es, "sq", [128, 1536], F32)
                qraw, B_qraw = sb(es, "qraw", [128, 768], F32)
                kvraw, B_kvraw = sb(es, "kvraw", [128, 1024], F32)
                post, B_post = sb(es, "post", [128, NPOST], BF16)
                ftb, B_ftb = sb(es, "ftb", [128, 28, 128], BF16)
                vblk, B_vblk = sb(es, "vblk", [128, 1, NV], BF16)
                ropet, B_ropet = sb(es, "ropet", [128, 4, 96], F32)
                cqn, B_cqn = sb(es, "cqn", [128, 384], BF16)
                cqT, B_cqT = sb(es, "cqT", [128, 3, 128], BF16)
                rt = [sb(es, f"rt{i}", [128, 512], F32) for i in range(4)]
                kb.op("pool", lambda e: e.memset(post[:], 0.0), writes=[B_post])
            if do_c:
                oTb, B_oTb = sb(es, "oTb", [128, 10, TB], BF16)
                mgT, B_mgT = sb(es, "mgT", [128, 8, TB], BF16)
                peT, B_peT = sb(es, "peT", [128, 2, TB], BF16)
                acc, B_acc = sb(es, "acc", [128, TB], F32)

            def wload(src_ap, view, src_buf):
                i = wctr[0] % 3
                wctr[0] += 1
                t, b = wsl[i]
                n = 1
                for d in view[1:]:
                    n *= d
                dst = t[:, 0:n]
                if len(view) == 3:
                    dst = dst.rearrange("p (a b) -> p a b", a=view[1])
                elif len(view) == 4:
                    dst = dst.rearrange("p (a b c) -> p a b c", a=view[1], b=view[2])
                kb.dma("sp", dst, src_ap, reads=[src_buf], writes=[b])
                return dst, b

            def norm_T(x, B_x, gc_idx, lidx):
                kb.op("dve", lambda e: e.memset(ss[:], 0.0), writes=[B_ss])
                for i in range(4):
                    kb.op("act", lambda e, i=i: e.activation(out=junk[:], in_=x[:, i, :], func=AF.Square,
                                                             accum_out=ss[:, i:i + 1]),
                          reads=[B_x], writes=[B_junk, B_ss])
                kb.op("act", lambda e: e.activation(out=rstd[:, 0:4], in_=ss[:, 0:4], func=AF.Sqrt,
                                                    bias=epsc[:, 0:1], scale=1.0 / D),
                      reads=[B_ss, B_eps], writes=[B_rstd])
                kb.op("dve", lambda e: e.reciprocal(out=rstd[:, 4:8], in_=rstd[:, 0:4]), reads=[B_rstd], writes=[B_rstd])
                for i in range(4):
                    kb.op("dve", lambda e, i=i: e.tensor_scalar(out=xn[:], in0=x[:, i, :], scalar1=rstd[:, 4 + i:5 + i],
                                                                scalar2=None, op0=ALU.mult),
                          reads=[B_x, B_rstd], writes=[B_xn])
                    b0, pb = ps_alloc(1)
                    pv = psf(b0).bitcast(BF16)
                    for c in range(8):
                        kb.op("pe", lambda e, c=c: e.transpose(pv[:, c * 128:(c + 1) * 128], xn[:, c * 128:(c + 1) * 128], ident[:]),
                              reads=[B_xn, B_ident], writes=pb, inc=(c == 7))
                    g = gcols[:, lidx, gc_idx * 8:(gc_idx + 1) * 8]
                    kb.op("dve", lambda e, i=i: e.tensor_tensor(
                        out=xT[:, :, i * 128:(i + 1) * 128], in0=pv[:, :].rearrange("p (c t) -> p c t", c=8),
                        in1=g.unsqueeze(2).to_broadcast([128, 8, 128]), op=ALU.mult),
                        reads=pb + [B_gcols], writes=[B_xT])

            def ffn(x, B_x, Win_d, Wout_d, lw):
                for jp in range(NF // 2):
                    wv, wb_ = wload(Win_d[2 * jp:2 * jp + 2].rearrange("j p c m -> p j c m"), [128, 2, 8, 256], B_wprep[lw])
                    for jj in range(2):
                        j = 2 * jp + jj
                        b0, pb = ps_alloc(2)
                        pa, pbb = psf(b0), psf(b0 + 1)
                        mm_group(pb[0:1], pa, [(wv[:, jj, c, 0:128], xT[:, c, :]) for c in range(8)], [wb_, B_xT])
                        mm_group(pb[1:2], pbb, [(wv[:, jj, c, 128:256], xT[:, c, :]) for c in range(8)], [wb_, B_xT])
                        tt, B_tt = tnh[j % 2]
                        tp, B_tp = tmp[j % 2]
                        kb.op("act", lambda e: e.activation(out=tt[:], in_=pa, func=AF.Tanh, scale=0.5),
                              reads=pb[0:1], writes=[B_tt])
                        kb.op("dve", lambda e: e.scalar_tensor_tensor(out=tp[:], in0=tt[:], scalar=1.0, in1=pa,
                                                                      op0=ALU.add, op1=ALU.mult),
                              reads=[B_tt] + pb[0:1], writes=[B_tp])
                        kb.op("dve", lambda e, j=j: e.tensor_tensor(out=hT[:, j, :], in0=tp[:], in1=pbb, op=ALU.mult),
                              reads=[B_tp] + pb[1:2], writes=[B_hT])
                for h in range(2):
                    b0, pb = ps_alloc(4)
                    for f0 in range(0, NF, 8):
                        nf = min(8, NF - f0)
                        wv, wb_ = wload(Wout_d[h, f0:f0 + nf].rearrange("f p m -> p f m"), [128, nf, 512], B_wprep[lw])
                        for i in range(4):
                            for f in range(nf):
                                ff = f0 + f
                                kb.op("pe", lambda e, i=i, f=f, ff=ff: e.matmul(
                                    psf(b0 + i), hT[:, ff, i * 128:(i + 1) * 128], wv[:, f, :],
                                    start=(ff == 0), stop=(ff == NF - 1)),
                                    reads=[B_hT, wb_], writes=pb[i:i + 1], inc=(f == nf - 1))
                    for i in range(4):
                        kb.op("dve", lambda e, i=i: e.scalar_tensor_tensor(
                            out=x[:, i, h * 512:(h + 1) * 512], in0=psf(b0 + i), scalar=0.25,
                            in1=x[:, i, h * 512:(h + 1) * 512], op0=ALU.mult, op1=ALU.add),
                            reads=pb[i:i + 1] + [B_x], writes=[B_x])

            def segnorm(src3, B_src, dst3, B_dst, G, n, gain_ap, tmp3=None):
                sq3 = sq[:, 0:G * n].rearrange("p (g n) -> p g n", g=G)
                kb.op("pool" if src3.tensor.name != "ps_all" else "dve",
                      lambda e: e.tensor_tensor(out=sq3, in0=src3, in1=src3, op=ALU.mult),
                      reads=[B_src], writes=[B_sq])
                kb.op("dve", lambda e: e.tensor_reduce(out=ss_big[:, 0:G],
                                                       in_=sq3, axis=AX.X, op=ALU.add),
                      reads=[B_sq], writes=[B_ssb])
                kb.op("act", lambda e: e.activation(out=ss_big[:, 32:32 + G], in_=ss_big[:, 0:G], func=AF.Sqrt,
                                                    bias=epsc[:, 0:1], scale=1.0 / n),
                      reads=[B_ssb, B_eps], writes=[B_ssb])
                kb.op("dve", lambda e: e.reciprocal(out=ss_big[:, 64:64 + G], in_=ss_big[:, 32:32 + G]),
                      reads=[B_ssb], writes=[B_ssb])
                t3 = sq3 if tmp3 is None else tmp3
                kb.op("dve", lambda e: e.tensor_tensor(out=t3, in0=src3,
                                                       in1=ss_big[:, 64:64 + G].unsqueeze(2).to_broadcast([128, G, n]),
                                                       op=ALU.mult),
                      reads=[B_src, B_ssb], writes=[B_sq])
                kb.op("pool", lambda e: e.tensor_tensor(out=dst3, in0=t3, in1=gain_ap, op=ALU.mult),
                      reads=[B_sq, B_gains], writes=[B_dst])

            def rope(src4, B_src, dst4, B_dst, cos3, sin3, shp):
                x1, x2 = src4[:, :, 0, :], src4[:, :, 1, :]
                n = shp[1] * 16
                tv = [rt[i][0][:, 0:n].rearrange("p (g h) -> p g h", h=16) for i in range(4)]
                tb_ = [rt[i][1] for i in range(4)]
                kb.op("dve", lambda e: e.tensor_tensor(out=tv[0], in0=x1, in1=cos3, op=ALU.mult), reads=[B_src, B_ropet], writes=[tb_[0]])
                kb.op("pool", lambda e: e.tensor_tensor(out=tv[1], in0=x2, in1=sin3, op=ALU.mult), reads=[B_src, B_ropet], writes=[tb_[1]])
                kb.op("dve", lambda e: e.tensor_tensor(out=tv[2], in0=x1, in1=sin3, op=ALU.mult), reads=[B_src, B_ropet], writes=[tb_[2]])
                kb.op("pool", lambda e: e.tensor_tensor(out=tv[3], in0=x2, in1=cos3, op=ALU.mult), reads=[B_src, B_ropet], writes=[tb_[3]])
                kb.op("dve", lambda e: e.tensor_tensor(out=dst4[:, :, 0, :], in0=tv[0], in1=tv[1], op=ALU.subtract),
                      reads=tb_[0:2], writes=[B_dst])
                kb.op("pool", lambda e: e.tensor_tensor(out=dst4[:, :, 1, :], in0=tv[2], in1=tv[3], op=ALU.add),
                      reads=tb_[2:4], writes=[B_dst])

            if do_a:
                ss_big, B_ssb = sb(es, "ss_big", [128, 96], F32)

            def phase_a_tail(blk, x, B_x):
                t0 = blk * TB
                kb.dma("sp", xs1[la][t0:t0 + TB, :].rearrange("(i p) d -> p i d", p=128), x[:], reads=[B_x], writes=[B_xs1[la]])
                kb.dma("sp", ropet[:], rope_in[t0:t0 + TB, :].rearrange("(i p) d -> p i d", p=128), reads=[B_in], writes=[B_ropet])
                s_ = max(i_ for i_ in range(NSEQ) if SEQ_OFF[i_] <= t0)
                ncs = SEQ_N[s_] // 128
                c0 = (t0 - SEQ_OFF[s_]) // 128
                dflat = Din[la][768:1408, :].rearrange("a b -> (a b)")
                xflat = Xin[la][768:1536, :].rearrange("a b -> (a b)")
                dense = dflat[SEQ_OFF[s_] * 640:SEQ_OFF[s_] * 640 + 10 * 128 * ncs * 64].rearrange("(h p c d) -> p c h d", h=10, p=128, c=ncs)
                dil = xflat[SEQ_OFF[s_] * 768:SEQ_OFF[s_] * 768 + 128 * ncs * 768].rearrange("(p c d) -> p c d", p=128, c=ncs)
                wvs = [(nb, 512 if nb < 6 else 416) for nb in range(7)]
                for pr in range(2):
                    for nb, wd in wvs:
                        wv, wb_ = wload(Win[la][nb], [128, 8, 512], B_wprep[la])
                        for ii in range(2):
                            i = 2 * pr + ii
                            b0, pb = ps_alloc(1)
                            mm_group(pb, psf(b0)[:, 0:wd], [(xT[:, c, i * 128:(i + 1) * 128], wv[:, c, 0:wd]) for c in range(8)],
                                     [B_xT, wb_])
                            kb.op("act", lambda e, nb=nb, wd=wd, b0=b0, ii=ii: e.copy(out=proj2[:, ii, nb * 512:nb * 512 + wd], in_=psf(b0)[:, 0:wd]),
                                  reads=pb, writes=[B_proj])
                    for ii in range(2):
                        post_tile(2 * pr + ii, proj2[:, ii, :], t0, dense, dil, c0)

            def post_tile(i, proj, t0, dense, dil, c0):
                if True:
                    segnorm(proj[:, 0:256].rearrange("p (g n) -> p g n", g=1), B_proj,
                            cqn[:, 0:256].rearrange("p (g n) -> p g n", g=1), B_cqn, 1, 256,
                            gains[:, G_CQ:G_CQ + 256].rearrange("p (g n) -> p g n", g=1))
                    segnorm(proj[:, 256:384].rearrange("p (g n) -> p g n", g=1), B_proj,
                            cqn[:, 256:384].rearrange("p (g n) -> p g n", g=1), B_cqn, 1, 128,
                            gains[:, G_CKV:G_CKV + 128].rearrange("p (g n) -> p g n", g=1))
                    b0, pb = ps_alloc(1)
                    pv = psf(b0).bitcast(BF16)
                    for c in range(3):
                        kb.op("pe", lambda e, c=c: e.transpose(pv[:, c * 128:(c + 1) * 128], cqn[:, c * 128:(c + 1) * 128], ident[:]),
                              reads=[B_cqn, B_ident], writes=pb, inc=(c == 2))
                    kb.op("dve", lambda e: e.tensor_copy(out=cqT[:], in_=pv[:, 0:384].rearrange("p (c t) -> p c t", c=3)),
                          reads=pb, writes=[B_cqT])
                    b0, pb = ps_alloc(2)
                    mm_group(pb[0:1], psf(b0), [(cqT[:, c, :], wuq[:, c, 0:512]) for c in range(2)], [B_cqT, B_wuq])
                    mm_group(pb[1:2], psf(b0 + 1)[:, 0:256], [(cqT[:, c, :], wuq[:, c, 512:768]) for c in range(2)], [B_cqT, B_wuq])
                    kb.op("act", lambda e, b0=b0: e.copy(out=qraw[:], in_=psf(b0, 2)[:, 0:768]), reads=pb, writes=[B_qraw])
                    b0, pb = ps_alloc(2)
                    mm_group(pb[0:1], psf(b0), [(cqT[:, 2, :], wukv[:, 0:512])], [B_cqT, B_wukv])
                    mm_group(pb[1:2], psf(b0 + 1), [(cqT[:, 2, :], wukv[:, 512:1024])], [B_cqT, B_wukv])
                    kb.op("act", lambda e, b0=b0: e.copy(out=kvraw[:], in_=psf(b0, 2)), reads=pb, writes=[B_kvraw])
                    q3 = qraw[:].rearrange("p (h d) -> p h d", h=8)
                    pq3 = post[:, QA0:QA0 + 768].rearrange("p (h d) -> p h d", h=8)
                    segnorm(q3[:, :, 0:64], B_qraw, pq3[:, :, 0:64], B_post, 8, 64,
                            gains[:, G_QA:G_QA + 64].unsqueeze(1).to_broadcast([128, 8, 64]))
                    nq = nrm[:, 0:256].rearrange("p (h d) -> p h d", h=8)
                    segnorm(q3[:, :, 64:96], B_qraw, nq, B_nrm, 8, 32,
                            gains[:, G_QA + 64:G_QA + 96].unsqueeze(1).to_broadcast([128, 8, 32]))
                    cosA = ropet[:, i, 0:16].unsqueeze(1).to_broadcast([128, 8, 16])
                    sinA = ropet[:, i, 16:32].unsqueeze(1).to_broadcast([128, 8, 16])
                    rope(nq.rearrange("p h (s d) -> p h s d", s=2), B_nrm,
                         pq3[:, :, 64:96].rearrange("p h (s d) -> p h s d", s=2), B_post, cosA, sinA, [128, 8, 16])
                    kv3 = kvraw[:].rearrange("p (h d) -> p h d", h=8)
                    segnorm(kv3[:, :, 0:64], B_kvraw, post[:, NQ + KAN0:NQ + KAN0 + 512].rearrange("p (h d) -> p h d", h=8), B_post,
                            8, 64, gains[:, G_KA:G_KA + 64].unsqueeze(1).to_broadcast([128, 8, 64]))
                    nk = nrm[:, 256:288].rearrange("p (h d) -> p h d", h=1)
                    segnorm(proj[:, 384:416].rearrange("p (h d) -> p h d", h=1), B_proj, nk, B_nrm, 1, 32,
                            gains[:, G_KA + 64:G_KA + 96].rearrange("p (h d) -> p h d", h=1))
                    rope(nk.rearrange("p h (s d) -> p h s d", s=2), B_nrm,
                         post[:, NQ + KAR0:NQ + KAR0 + 32].rearrange("p (h s d) -> p h s d", h=1, s=2), B_post,
                         ropet[:, i, 0:16].unsqueeze(1), ropet[:, i, 16:32].unsqueeze(1), [128, 1, 16])
                    segnorm(proj[:, 416:416 + 768].rearrange("p (h d) -> p h d", h=12), B_proj,
                            post[:, QB0:QB0 + 768].rearrange("p (h d) -> p h d", h=12), B_post, 12, 64,
                            gains[:, G_B:G_B + 768].rearrange("p (h d) -> p h d", h=12))
                    segnorm(proj[:, 1184:1184 + 768].rearrange("p (h d) -> p h d", h=12), B_proj,
                            post[:, NQ + KB0:NQ + KB0 + 768].rearrange("p (h d) -> p h d", h=12), B_post, 12, 64,
                            gains[:, G_B + 768:G_B + 1536].rearrange("p (h d) -> p h d", h=12))
                    nc3 = nrm[:, 320:320 + 640].rearrange("p (h d) -> p h d", h=10)
                    segnorm(proj[:, 2720:2720 + 640].rearrange("p (h d) -> p h d", h=10), B_proj, nc3, B_nrm, 10, 64,
                            gains[:, G_C:G_C + 640].rearrange("p (h d) -> p h d", h=10))
                    cosC = ropet[:, i, 32:64].rearrange("p (s d) -> p s d", s=2)
                    sinC = ropet[:, i, 64:96].rearrange("p (s d) -> p s d", s=2)
                    for (h0, nh, dstc) in ((0, 8, QC0), (8, 2, NQ + KC0)):
                        for sg in range(2):
                            sv = nrm[:, 320 + h0 * 64:320 + (h0 + nh) * 64].rearrange("p (h g s d) -> p h g s d", g=2, s=2, d=16)[:, :, sg]
                            dv = post[:, dstc:dstc + nh * 64].rearrange("p (h g s d) -> p h g s d", g=2, s=2, d=16)[:, :, sg]
                            rope(sv, B_nrm, dv, B_post,
                                 cosC[:, sg, :].unsqueeze(1).to_broadcast([128, nh, 16]),
                                 sinC[:, sg, :].unsqueeze(1).to_broadcast([128, nh, 16]), [128, nh, 16])
                    kb.op("act", lambda e: e.copy(out=vblk[:, 0, VA0:VA0 + 512].rearrange("p (h d) -> p h d", h=8), in_=kv3[:, :, 64:128]),
                          reads=[B_kvraw], writes=[B_vblk])
                    kb.op("act", lambda e: e.copy(out=vblk[:, 0, VC0:VC0 + 128], in_=proj[:, 3360:3488]), reads=[B_proj], writes=[B_vblk])
                    kb.op("act", lambda e: e.copy(out=vblk[:, 0, VB0:VB0 + 768], in_=proj[:, 1952:2720]), reads=[B_proj], writes=[B_vblk])
                    for g0 in range(0, 28, 8):
                        ng = min(8, 28 - g0)
                        b0, pb = ps_alloc(1)
                        pv = psf(b0).bitcast(BF16)
                        for c in range(ng):
                            kb.op("pe", lambda e, c=c, g0=g0: e.transpose(pv[:, c * 128:(c + 1) * 128],
                                                                         post[:, (g0 + c) * 128:(g0 + c + 1) * 128], ident[:]),
                                  reads=[B_post, B_ident], writes=pb, inc=(c == ng - 1))
                        kb.op("act", lambda e, g0=g0, ng=ng: e.copy(out=ftb[:, g0:g0 + ng, :],
                                                                    in_=pv[:, 0:ng * 128].rearrange("p (c t) -> p c t", c=ng)),
                              reads=pb, writes=[B_ftb])
                    tt0 = t0 + i * 128
                    kb.dma("sp", QT[la].rearrange("(b p) t -> p b t", p=128)[:, :, tt0:tt0 + 128], ftb[:, 0:16, :], reads=[B_ftb], writes=[B_QT[la]])
                    kb.dma("sp", Xin[la][0:768, :].rearrange("(b p) t -> p b t", p=128)[:, :, tt0:tt0 + 128], ftb[:, 16:22, :], reads=[B_ftb], writes=[B_gKin[la]])
                    kb.dma("sp", Din[la][0:768, :].rearrange("(b p) t -> p b t", p=128)[:, :, tt0:tt0 + 128], ftb[:, 22:28, :], reads=[B_ftb], writes=[B_gKin[la]])
                    kb.dma("sp", dense[:, c0 + i:c0 + i + 1], vblk[:, :, 0:640].rearrange("p c (h d) -> p c h d", h=10), reads=[B_vblk], writes=[B_gVin[la]])
                    kb.dma("sp", dil[:, c0 + i:c0 + i + 1, :], vblk[:, :, 640:1408], reads=[B_vblk], writes=[B_gVin[la]])

            def phase_c(blk, x, B_x):
                t0 = blk * TB
                kb.dma("sp", x[:], xs1[l][t0:t0 + TB, :].rearrange("(i p) d -> p i d", p=128), reads=[B_xs1[l]], writes=[B_x])
                kb.dma("sp", oTb[:], oT[l].rearrange("(c p) t -> p c t", p=128)[:, :, t0:t0 + TB], reads=[B_oT[l]], writes=[B_oTb])
                kb.dma("sp", peT[:], pTb[l].rearrange("(c p) t -> p c t", p=128)[:, :, t0:t0 + TB], reads=[B_wprep[l]], writes=[B_peT])
                norm_T(x, B_x, 1, l)
                wg_v = Wgate[l].rearrange("(b j) p c m -> j p b (c m)", b=3)
                brk = ((0, 4), (4, 6), (6, 10))
                for j in range(8):
                    wbv, wbb = wload(Wbr[l][j], [128, 10, 128], B_wprep[l])
                    wgv, wgb = wload(wg_v[j], [128, 3, 1024], B_wprep[l])
                    for br in range(3):
                        b0, pb = ps_alloc(2)
                        mm_group(pb[0:1], psf(b0), [(wgv[:, br, c * 128:(c + 1) * 128], xT[:, c, :]) for c in range(8)], [wgb, B_xT])
                        c0_, c1_ = brk[br]
                        mm_group(pb[1:2], psf(b0 + 1), [(wbv[:, c, :], oTb[:, c, :]) for c in range(c0_, c1_)], [wbb, B_oTb])
                        tt, B_tt = tnh[br % 2]
                        tp, B_tp = tmp[br % 2]
                        kb.op("act", lambda e, br=br: e.activation(out=tt[:], in_=psf(b0), func=AF.Tanh, scale=0.5,
                                                                   bias=bgh[:, 8 * br + j:8 * br + j + 1]),
                              reads=pb[0:1] + [B_bgh], writes=[B_tt])
                        if br == 0:
                            kb.op("dve", lambda e: e.scalar_tensor_tensor(out=acc[:], in0=tt[:], scalar=1.0, in1=psf(b0 + 1),
                                                                          op0=ALU.add, op1=ALU.mult),
                                  reads=[B_tt] + pb[1:2], writes=[B_acc])
                        else:
                            kb.op("dve", lambda e: e.scalar_tensor_tensor(out=tp[:], in0=tt[:], scalar=1.0, in1=psf(b0 + 1),
                                                                          op0=ALU.add, op1=ALU.mult),
                                  reads=[B_tt] + pb[1:2], writes=[B_tp])
                            if br == 1:
                                kb.op("pool", lambda e: e.tensor_tensor(out=acc[:], in0=acc[:], in1=tp[:], op=ALU.add),
                                      reads=[B_acc, B_tp], writes=[B_acc])
                            else:
                                kb.op("pool", lambda e: e.tensor_tensor(out=mgT[:, j, :], in0=acc[:], in1=tp[:], op=ALU.add),
                                      reads=[B_acc, B_tp], writes=[B_mgT])
                for h in range(2):
                    wv, wb_ = wload(Wout[l][h], [128, 8, 512], B_wprep[l])
                    for i in range(4):
                        b0, pb = ps_alloc(1)
                        mm_group(pb, psf(b0), [(mgT[:, c, i * 128:(i + 1) * 128], wv[:, c, :]) for c in range(8)], [B_mgT, wb_])
                        kb.op("dve", lambda e, i=i, b0=b0: e.scalar_tensor_tensor(
                            out=x[:, i, h * 512:(h + 1) * 512], in0=psf(b0), scalar=0.5,
                            in1=x[:, i, h * 512:(h + 1) * 512], op0=ALU.mult, op1=ALU.add),
                            reads=pb + [B_x], writes=[B_x])
                norm_T(x, B_x, 2, l)
                ffn(x, B_x, W2in[l], W2out[l], l)
                norm_T(x, B_x, 3, l)
                for h in range(2):
                    wpg_v, wpg_b = wload(Wpg[l][h], [128, 8, 512], B_wprep[l])
                    wpl_v, wpl_b = wload(Wple[l][h], [128, 2, 512], B_wprep[l])
                    for i in range(4):
                        b0, pb = ps_alloc(2)
                        mm_group(pb[0:1], psf(b0), [(xT[:, c, i * 128:(i + 1) * 128], wpg_v[:, c, :]) for c in range(8)], [B_xT, wpg_b])
                        mm_group(pb[1:2], psf(b0 + 1), [(peT[:, c, i * 128:(i + 1) * 128], wpl_v[:, c, :]) for c in range(2)], [B_peT, wpl_b])
                        tt, B_tt = tnh[i % 2]
                        tp, B_tp = tmp[i % 2]
                        kb.op("act", lambda e: e.activation(out=tt[:], in_=psf(b0), func=AF.Tanh, scale=0.5), reads=pb[0:1], writes=[B_tt])
                        kb.op("dve", lambda e: e.scalar_tensor_tensor(out=tp[:], in0=tt[:], scalar=1.0, in1=psf(b0 + 1),
                                                                      op0=ALU.add, op1=ALU.mult),
                              reads=[B_tt] + pb[1:2], writes=[B_tp])
                        kb.op("dve", lambda e, i=i: e.scalar_tensor_tensor(
                            out=x[:, i, h * 512:(h + 1) * 512], in0=tp[:], scalar=0.5,
                            in1=x[:, i, h * 512:(h + 1) * 512], op0=ALU.mult, op1=ALU.add),
                            reads=[B_tp, B_x], writes=[B_x])
                if last:
                    kb.dma("sp", y_out[t0:t0 + TB, :].rearrange("(i p) d -> p i d", p=128), x[:], reads=[B_x], writes=[B_y])

            for blk in range(NBLK):
                t0 = blk * TB
                x, B_x = xb[0]
                if do_c:
                    phase_c(blk, x, B_x)
                else:
                    kb.dma("sp", x[:], x_in[t0:t0 + TB, :].rearrange("(i p) d -> p i d", p=128), reads=[B_in], writes=[B_x])
                if do_a:
                    norm_T(x, B_x, 0, la)
                    ffn(x, B_x, W1in[la], W1out[la], la)
                    norm_T(x, B_x, 1, la)
                    phase_a_tail(blk, x, B_x)
                if stage == "A" and blk == 0:
                    break
                if stage in ("B", "B2", "B3") and blk == 0:
                    break
            kb.end_phase()


    def ps_ring(lo, hi, n=1):
        if kb.ps_next < lo or kb.ps_next + n > hi:
            kb.ps_next = lo
        b0 = kb.ps_next
        kb.ps_next = b0 + n
        return b0, B_ps[b0:b0 + n]

    def gather(l):
        kb.coll(lambda e: e.collective_compute("AllGather", ALU.bypass, replica_groups=[list(range(NCORES))],
                                               ins=[Xin[l]], outs=[Xout]),
                reads=[B_gKin[l], B_gVin[l]], writes=[B_gKout[l]])
        kb.coll(lambda e: e.collective_compute("AllGather", ALU.bypass, replica_groups=[list(range(NCORES))],
                                               ins=[Din[l]], outs=[Dout]),
                reads=[B_gKin[l], B_gVin[l]], writes=[B_gVout[l]])

    def attn_dense(l, seqs):
        gKv = Dout.rearrange("(r f) t -> r f t", r=NCORES)
        gVf = Dout.rearrange("(r f) t -> r (f t)", r=NCORES)[:, 768 * T:]
        DK = 768
        with contextlib.ExitStack() as es:
            KT = [sb(es, f"KT{i}", [128, 16384], BF16) for i in range(2)]
            V1 = [sb(es, f"V1{i}", [128, 128, 128], BF16) for i in range(2)]
            QTt = [sb(es, f"QTt{i}", [128, 512], BF16) for i in range(3)]
            PT = [sb(es, f"PT{i}", [128, 512], BF16) for i in range(4)]
            rc = [sb(es, f"rc{i}", [64, 512], F32) for i in range(2)]
            on = [sb(es, f"on{i}", [64, 512], BF16) for i in range(2)]
            for i in range(2):
                kb.op("pool", lambda e, i=i: e.memset(V1[i][0][:, :, 64:128], 1.0), writes=[V1[i][1]])
                kb.op("pool", lambda e, i=i: e.memset(KT[i][0][:], 0.0), writes=[KT[i][1]])
            for i in range(3):
                kb.op("pool", lambda e, i=i: e.memset(QTt[i][0][:], 0.0), writes=[QTt[i][1]])
            ucnt = 0
            qcnt = 0
            for s in seqs:
                n, S, off = SEQ_N[s], SEQ_S[s], SEQ_OFF[s]
                ncs = n // 128
                nch = S // 128
                units = [("a", h) for h in range(8)] + [("c", kv) for kv in range(2)]
                for kind, u in units:
                    kt, B_kt = KT[ucnt % 2]
                    v1, B_v1 = V1[ucnt % 2]
                    ucnt += 1
                    if kind == "a":
                        dk, dkm, scale = 96, 128, 96 ** -0.5
                        kb.dma("sp", kt[0:64, 0:S].rearrange("p (r t) -> p r t", r=NCORES),
                               gKv[:, KAN0 - DK + 64 * u:KAN0 - DK + 64 * u + 64, off:off + n].rearrange("r f t -> f r t"),
                               reads=[B_gVout[l]], writes=[B_kt])
                        kb.dma("sp", kt[64:96, 0:S].rearrange("p (r t) -> p r t", r=NCORES),
                               gKv[:, KAR0 - DK:KAR0 - DK + 32, off:off + n].rearrange("r f t -> f r t"),
                               reads=[B_gVout[l]], writes=[B_kt], part=True)
                        hd = u
                        qheads = [(QA0 + 96 * u, 64 * u)]
                    else:
                        dk, dkm, scale = 64, 64, 0.125
                        kb.dma("sp", kt[0:64, 0:S].rearrange("p (r t) -> p r t", r=NCORES),
                               gKv[:, KC0 - DK + 64 * u:KC0 - DK + 64 * u + 64, off:off + n].rearrange("r f t -> f r t"),
                               reads=[B_gVout[l]], writes=[B_kt])
                        hd = 8 + u
                        qheads = [(QC0 + 64 * (4 * u + g), 768 + 64 * (4 * u + g)) for g in range(4)]
                    voff = off * 640 + hd * 128 * ncs * 64
                    for r in range(NCORES):
                        kb.dma("sp", v1[:, r * ncs:(r + 1) * ncs, 0:64],
                               gVf[r, voff:voff + 128 * ncs * 64].rearrange("(p c d) -> p c d", p=128, c=ncs),
                               reads=[B_gVout[l]], writes=[B_v1], part=(r > 0))
                    for (qrow, orow) in qheads:
                        for qt in range(n // 512):
                            tok0 = off + qt * 512
                            q_, B_q = QTt[qcnt % 3]
                            rc_, B_rc = rc[qcnt % 2]
                            on_, B_on = on[qcnt % 2]
                            ob = 6 + (qcnt % 2)
                            qcnt += 1
                            kb.dma("sp", q_[0:dk, :], QT[l][qrow:qrow + dk, tok0:tok0 + 512], reads=[B_QT[l]], writes=[B_q])
                            LOOK = 2
                            sb_of = {}
                            for c in range(nch + LOOK):
                                if c < nch:
                                    b0, pb = ps_ring(0, 6, 1)
                                    sb_of[c] = (b0, pb)
                                    kb.op("pe", lambda e, b0=b0, c=c: e.matmul(psf(b0), kt[0:dkm, c * 128:(c + 1) * 128], q_[0:dkm, :],
                                                                               start=True, stop=True),
                                          reads=[B_kt, B_q], writes=pb)
                                    pt, B_pt = PT[c % 4]
                                    kb.op("act", lambda e, b0=b0, pt=pt: e.activation(out=pt[:], in_=psf(b0), func=AF.Exp, scale=scale),
                                          reads=pb, writes=[B_pt])
                                cc = c - LOOK
                                if cc >= 0:
                                    pt, B_pt = PT[cc % 4]
                                    kb.op("pe", lambda e, cc=cc, pt=pt: e.matmul(psf(ob), v1[:, cc, :], pt[:],
                                                                                 start=(cc == 0), stop=(cc == nch - 1)),
                                          reads=[B_v1, B_pt], writes=B_ps[ob:ob + 1], inc=(cc == nch - 1))
                            kb.op("dve", lambda e: e.reciprocal(out=rc_[0:64, :], in_=psf(ob)[64:128, :]),
                                  reads=B_ps[ob:ob + 1], writes=[B_rc])
                            kb.op("dve", lambda e: e.tensor_tensor(out=on_[0:64, :], in0=psf(ob)[0:64, :], in1=rc_[0:64, :], op=ALU.mult),
                                  reads=B_ps[ob:ob + 1] + [B_rc], writes=[B_on])
                            kb.dma("sp", oT[l][orow:orow + 64, tok0:tok0 + 512], on_[0:64, :], reads=[B_on], writes=[B_oT[l]])
            kb.end_phase()

    halo_rv = []
    B_hreg = Buf("hreg", dram=True)

    def attn_dil(l, seqs):
        gXv = Xout.rearrange("(r f) t -> r f t", r=NCORES)
        gVin_f = Xin[l][768:1536, :].rearrange("a b -> (a b)")
        with contextlib.ExitStack() as es:
            NE = 32
            Gt, B_Gt = sb(es, "Gt", [128, 4, GTOT], BF16)
            gst, B_gst = sb(es, "gst", [128, GTOT], F32)
            Kx, B_Kx = sb(es, "Kx", [128, 6, NE * 128], BF16)
            Vx, B_Vx = sb(es, "Vx", [128, NE, 768], BF16)
            vones, B_vones = sb(es, "vones", [128, NE, 64], BF16)
            Qd, B_Qd = sb(es, "Qd", [128, 6, 2048], BF16)
            kval, B_kval = sb(es, "kval", [128, KV_OFF[-1]], F32)
            Es = [sb(es, f"Es{i}", [128, GTOT], BF16) for i in range(2)]
            Pt = [sb(es, f"Pt{i}", [128, GTOT], BF16) for i in range(2)]
            rcd, B_rcd = sb(es, "rcd", [64, 512], F32)
            ond, B_ond = sb(es, "ond", [64, 512], BF16)
            kb.dma("sp", kval[:], kvalid_in, reads=[B_in], writes=[B_kval])
            for kk in range(4):
                kb._wait("pool", kb._deps([B_gKout[l]], []))
                if len(halo_rv) <= kk:
                    hr = nc.gpsimd.alloc_register(f"hreg{kk}")
                    kb.op("pool", lambda e: e.reg_load(hr, hidx_in[0:1, kk:kk + 1]), reads=[B_in], writes=[B_hreg])
                    halo_rv.append(hr)
                kb._wait("pool", kb._deps([B_hreg], []))
                rv = nc.gpsimd.snap(halo_rv[kk], min_val=0, max_val=NCORES - 1)
                kb.dma("pool", nbr[kk], gXv[bass.ds(rv, 1), :, :].rearrange("a f t -> (a f) t"),
                       reads=[B_gKout[l]], writes=[B_nbr[l]])
            for j in range(4):
                kb.dma("sp", gst[:], gb_in[:, j, :], reads=[B_in], writes=[B_gst])
                kb.op("act", lambda e, j=j: e.activation(out=Gt[:, j, :], in_=gst[:], func=AF.Exp), reads=[B_gst], writes=[B_Gt])
            cnt = 0
            for s in seqs:
                n, S, off = SEQ_N[s], SEQ_S[s], SEQ_OFF[s]
                ncs = n // 128
                next_ = ncs + 16
                kb.op("pool", lambda e: e.memset(vones[:], 1.0), writes=[B_vones])
                kb.dma("sp", Kx[:, :, HALO:HALO + n], Xin[l][0:768, off:off + n].rearrange("(q p) t -> p q t", p=128),
                       reads=[B_gKin[l]], writes=[B_Kx], part=True)
                dbase = off * 768
                kb.dma("sp", Vx[:, 8:8 + ncs, 0:768], gVin_f[dbase:dbase + 128 * ncs * 768].rearrange("(p c d) -> p c d", p=128, c=ncs),
                       reads=[B_gVin[l]], writes=[B_Vx], part=True)
                kb.dma("sp", Qd[:, :, 0:n], QT[l][QB0:QB0 + 768, off:off + n].rearrange("(q p) t -> p q t", p=128),
                       reads=[B_QT[l]], writes=[B_Qd])
                for k in range(4):
                    if n == 512:
                        loc = 0
                    else:
                        loc = (1024 + 512 * k) if k < 2 else 512 * (k - 2)
                    xpos = 512 * k if k < 2 else HALO + n + 512 * (k - 2)
                    kk = k if n == 512 else (1 if k < 2 else 2)
                    srcK = nbrK[l][kk][:, off + loc:off + loc + 512].rearrange("(q p) t -> p q t", p=128)
                    kb.dma("sp", Kx[:, :, xpos:xpos + 512], srcK, reads=[B_nbr[l]], writes=[B_Kx], part=True)
                    srcV = nbrV[l][kk][dbase:dbase + 128 * ncs * 768].rearrange("(p c d) -> p c d", p=128, c=ncs)
                    kb.dma("sp", Vx[:, xpos // 128:xpos // 128 + 4, 0:768], srcV[:, loc // 128:loc // 128 + 4, :],
                           reads=[B_nbr[l]], writes=[B_Vx], part=True)
                for (e0, e1) in ((0, 8), (8 + ncs, 16 + ncs)):
                    kb.op("dve", lambda e, e0=e0, e1=e1: e.tensor_tensor(
                        out=Vx[:, e0:e1, :], in0=Vx[:, e0:e1, :],
                        in1=kval[:, KV_OFF[s] + e0:KV_OFF[s] + e1].unsqueeze(2).to_broadcast([128, 8, 768]), op=ALU.mult),
                        reads=[B_Vx, B_kval], writes=[B_Vx])
                    kb.op("dve", lambda e, e0=e0, e1=e1: e.tensor_tensor(
                        out=vones[:, e0:e1, :], in0=vones[:, e0:e1, :],
                        in1=kval[:, KV_OFF[s] + e0:KV_OFF[s] + e1].unsqueeze(2).to_broadcast([128, 8, 64]), op=ALU.mult),
                        reads=[B_vones, B_kval], writes=[B_vones])
                if stage == "B3":
                    B_dbg2 = Buf("dbg2", dram=True)
                    dK = dout("dbg_Kx", [128, 6, 3072], BF16)
                    dV = dout("dbg_Vx", [128, 24, 768], BF16)
                    dO = dout("dbg_vones", [128, 24, 64], BF16)
                    dG = dout("dbg_Gt", [128, 4, GTOT], BF16)
                    kb.dma("sp", dK, Kx[:, :, 0:3072], reads=[B_Kx], writes=[B_dbg2])
                    kb.dma("sp", dV, Vx[:, 0:24, :], reads=[B_Vx], writes=[B_dbg2])
                    kb.dma("sp", dO, vones[:, 0:24, :], reads=[B_vones], writes=[B_dbg2])
                    kb.dma("sp", dG, Gt[:], reads=[B_Gt], writes=[B_dbg2])
                for ti in range(ncs):
                    for j in range(4):
                        es_, B_es = Es[cnt % 2]
                        pt_, B_pt = Pt[cnt % 2]
                        cnt += 1
                        blocks = []
                        for g, (d, nh) in enumerate(DIL):
                            hq = g * 4 + j
                            pair, half = hq // 2, hq % 2
                            for jj in range(2 * nh + 1):
                                jw = 2 * nh - jj
                                e_ = ti + 8 - nh + jw
                                col = GOFF[g] + jj * 128
                                pcol = (col - 1024) if g == 2 else (5 * 512 + col)
                                bank = pcol // 512
                                last = (g == 2 and jj == 2 * nh)
                                kb.op("pe", lambda e, pair=pair, half=half, e_=e_, pcol=pcol: e.matmul(
                                    ps_all[:, pcol:pcol + 128], Kx[64 * half:64 * half + 64, pair, e_ * 128:(e_ + 1) * 128],
                                    Qd[64 * half:64 * half + 64, pair, ti * 128:(ti + 1) * 128], start=True, stop=True),
                                    reads=[B_Kx, B_Qd], writes=[B_ps[bank]], inc=(jj == 2 * nh))
                                blocks.append((hq, e_, col))
                        kb.op("act", lambda e: e.activation(out=es_[:, 0:1024], in_=ps_all[:, 5 * 512:7 * 512], func=AF.Exp, scale=0.125),
                              reads=B_ps[5:7], writes=[B_es])
                        kb.op("act", lambda e: e.activation(out=es_[:, 1024:GTOT], in_=ps_all[:, 0:2176], func=AF.Exp, scale=0.125),
                              reads=B_ps[0:5], writes=[B_es])
                        kb.op("dve", lambda e, j=j: e.tensor_tensor(out=pt_[:], in0=es_[:], in1=Gt[:, j, :], op=ALU.mult),
                              reads=[B_es, B_Gt], writes=[B_pt])
                        nb_ = len(blocks)
                        for bi, (hq, e_, col) in enumerate(blocks):
                            kb.op("pe", lambda e, col=col, bi=bi, hq=hq, e_=e_: e.matmul(
                                ps_all[0:64, 7 * 512 + j * 128:7 * 512 + (j + 1) * 128], Vx[:, e_, hq * 64:(hq + 1) * 64],
                                pt_[:, col:col + 128], start=(bi == 0), stop=(bi == nb_ - 1)),
                                reads=[B_Vx, B_pt], writes=[B_ps[7]], inc=False)
                            kb.op("pe", lambda e, col=col, bi=bi, e_=e_: e.matmul(
                                ps_all[64:128, 7 * 512 + j * 128:7 * 512 + (j + 1) * 128], vones[:, e_, :],
                                pt_[:, col:col + 128], start=(bi == 0), stop=(bi == nb_ - 1), tile_position=(0, 64)),
                                reads=[B_vones, B_pt], writes=[B_ps[7]], inc=(bi == nb_ - 1))
                    kb.op("dve", lambda e: e.reciprocal(out=rcd[0:64, :], in_=ps_all[64:128, 7 * 512:8 * 512]),
                          reads=[B_ps[7]], writes=[B_rcd])
                    kb.op("dve", lambda e: e.tensor_tensor(out=ond[0:64, :], in0=ps_all[0:64, 7 * 512:8 * 512], in1=rcd[0:64, :], op=ALU.mult),
                          reads=[B_ps[7], B_rcd], writes=[B_ond])
                    tok0 = off + ti * 128
                    kb.dma("sp", oT[l][512:768, tok0:tok0 + 128].rearrange("(j d) t -> d j t", j=4),
                           ond[0:64, :].rearrange("d (j t) -> d j t", j=4), reads=[B_ond], writes=[B_oT[l]])
            kb.end_phase()

    token_phase(0, False, True, False)
    kb.barrier()
    if stage == "A":
        B_dbg = Buf("dbg", dram=True)
        d1 = dout("dbg_xs1", [512, D])
        d2 = dout("dbg_QT", [NQ, 512], BF16)
        d3 = dout("dbg_K", [NK, 512], BF16)
        d4 = dout("dbg_V", [512, NV], BF16)
        kb.dma("sp", d1, xs1[0][0:512, :], reads=[B_xs1[0]], writes=[B_dbg])
        kb.dma("sp", d2, QT[0][:, 0:512], reads=[B_QT[0]], writes=[B_dbg])
        kb.dma("sp", d3, gKin[0][:, 0:512], reads=[B_gKin[0]], writes=[B_dbg])
        kb.dma("sp", d4, gVin[0][0:512, :], reads=[B_gVin[0]], writes=[B_dbg])
        kb.barrier()
        es_top.close()
        return nc
    if stage in ("B", "B2", "B3"):
        gather(0)
        if stage in ("B", "B3"):
            attn_dil(0, [0])
        if stage in ("B", "B2"):
            attn_dense(0, [0])
        B_dbg = Buf("dbg", dram=True)
        d1 = dout("dbg_oT", [1280, 512], BF16)
        kb.dma("sp", d1, oT[0][:, 0:512], reads=[B_oT[0]], writes=[B_dbg])
        kb.barrier()
        es_top.close()
        return nc
    for l in range(L):
        if l > 0:
            kb.rotate_sems()
        gather(l)
        if stage == "L1g" and l == 1:
            break
        attn_dil(l, list(range(NSEQ)))
        if stage == "L1b" and l == 1:
            break
        attn_dense(l, list(range(NSEQ)))
        token_phase(l, True, l + 1 < L, l + 1 == L)
        if stage == "L1a":
            break
    kb.barrier()
    es_top.close()
    return nc


def t5_bucket(rel):
    nb = 16
    max_exact = 8
    n = np.abs(rel)
    large = max_exact + (np.log(np.maximum(n, 1) / max_exact) / np.log(1024 / max_exact) * (nb - max_exact)).astype(np.int32)
    large = np.minimum(large, nb - 1)
    return (rel > 0).astype(np.int32) * nb + np.where(n < max_exact, n, large).astype(np.int32)


def host_inputs(inp):
    f32 = np.float32
    xs = [inp["x_prompt"][i] for i in range(4)] + [inp["x_sample"][i] for i in range(2)]
    ps = [inp["p_prompt"][:, i] for i in range(4)] + [inp["p_sample"][:, i] for i in range(2)]
    freqs = (10000.0 ** (-np.arange(16) / 16)).astype(f32)
    ident = np.eye(128, dtype=f32)
    rb = np.asarray(inp["rel_bias"], f32)
    gb = np.full((128, 4, GTOT), -30000.0, f32)
    for g, (d, nh) in enumerate(DIL):
        kk = np.arange(128)[:, None]
        m = np.arange(GW[g])[None, :]
        delta = kk - m + 128 * nh
        ok = (delta % d == 0) & (np.abs(delta) <= 64 * d)
        bk = t5_bucket(delta)
        for h in range(4):
            vals = rb[bk, g * 4 + h]
            gb[:, h, GOFF[g]:GOFF[g] + GW[g]] = np.where(ok, vals, f32(-30000.0))
    gcols = np.stack([np.concatenate([np.asarray(inp[k][l], f32).reshape(8, 128).T for k in ("g_ffn1", "g_mix", "g_ffn2", "g_ple")], axis=1)
                      for l in range(DEPTH)])
    bgc = np.stack([np.asarray(inp["b_gate"][l], f32).reshape(24, 128).T for l in range(DEPTH)])
    gains = np.stack([np.concatenate([inp["g_cq"][l], inp["g_ckv"][l], np.tile(inp["g_qb"][l], 12), np.tile(inp["g_kb"][l], 12),
                                      np.tile(inp["g_qc"][l], 8), np.tile(inp["g_kc"][l], 2), inp["g_qa"][l], inp["g_ka"][l]]).astype(f32)
                      for l in range(DEPTH)])
    maps = []
    for c in range(NCORES):
        xl = np.concatenate([xs[s][c * SEQ_N[s]:(c + 1) * SEQ_N[s]] for s in range(NSEQ)], axis=0)
        pl = np.concatenate([ps[s][:, c * SEQ_N[s]:(c + 1) * SEQ_N[s]] for s in range(NSEQ)], axis=1)
        pT = np.ascontiguousarray(np.transpose(pl, (0, 2, 1)))
        pos = np.concatenate([np.arange(c * SEQ_N[s], (c + 1) * SEQ_N[s]) for s in range(NSEQ)])
        rope = np.zeros((T, 96), f32)
        angA = pos.astype(f32)[:, None] * freqs[None, :]
        angR = (pos // 64).astype(f32)[:, None] * freqs[None, :]
        angC = (pos % 64).astype(f32)[:, None] * freqs[None, :]
        rope[:, 0:16] = np.cos(angA)
        rope[:, 16:32] = np.sin(angA)
        rope[:, 32:48] = np.cos(angR)
        rope[:, 48:64] = np.cos(angC)
        rope[:, 64:80] = np.sin(angR)
        rope[:, 80:96] = np.sin(angC)
        kvalid = np.zeros((128, KV_OFF[-1]), f32)
        hidx = np.zeros((1, 24), np.int32)
        for s in range(NSEQ):
            n = SEQ_N[s]
            for e in range(NEXT_CH[s]):
                gpos = c * n - HALO + e * 128
                kvalid[:, KV_OFF[s] + e] = 1.0 if (0 <= gpos < SEQ_S[s]) else 0.0
            for k in range(4):
                gpos = c * n - HALO + 512 * k if k < 2 else (c + 1) * n + 512 * (k - 2)
                r = gpos // n
                hidx[0, s * 4 + k] = min(max(r, 0), NCORES - 1)
        m = {"x_loc": np.ascontiguousarray(xl, f32), "pT_loc": pT.astype(f32), "rope_tab": rope, "ident": ident,
             "gb_tab": gb, "kvalid": kvalid, "hidx": hidx, "gcols": gcols, "bgate_cols": bgc, "gains_row": gains}
        for k in ("w_ffn1_in", "w_ffn1_out", "w_in", "w_uq", "w_ukv", "w_gate", "w_oa", "w_ob", "w_oc", "w_out",
                  "w_ffn2_in", "w_ffn2_out", "w_pg", "w_ple"):
            m[k] = np.ascontiguousarray(inp[k], f32)
        maps.append(m)
    return maps


_NC_CACHE = {}


def kernel(**inputs):
    inp = {k: np.asarray(v) for k, v in inputs.items()}
    maps = host_inputs(inp)
    if "nc" not in _NC_CACHE:
        _NC_CACHE["nc"] = build()
    nc = _NC_CACHE["nc"]
    res = run_bass_kernel_spmd(nc, maps, core_ids=list(range(NCORES)))
    ys = [np.asarray(r["y_loc"], np.float32) for r in res.results]
    y_prompt = np.zeros((4, 4096, D), np.float32)
    y_sample = np.zeros((2, 16384, D), np.float32)
    for c in range(NCORES):
        for s in range(NSEQ):
            blk = ys[c][SEQ_OFF[s]:SEQ_OFF[s] + SEQ_N[s]]
            if s < 4:
                y_prompt[s, c * 512:(c + 1) * 512] = blk
            else:
                y_sample[s - 4, c * 2048:(c + 1) * 2048] = blk
    return (y_prompt, y_sample)
```
